# Optimizing a Trainium2 kernel written in Bass

```python
import math
import jax
import jax.numpy as jnp
from jax import lax
import numpy as np

D_MODEL = 1024
BATCH = 1
SEQ = 16384
DEPTH = 2

GRID_W = 64
CTX_LEN = 256
HEAD_DIM = 64
Q_BLOCK = 128
EPS = 1e-6
ROPE_THETA = 10000.0
D_FF = 2816
ADA_STD = 0.02
N_MOD = 9

A_HEADS = 8
A_KV_HEADS = 2
B_HEADS = 8
B_NOPE = 64
B_ROPE = 32
B_VDIM = 64
B_KV_RANK = 256
B_QK = B_NOPE + B_ROPE
C_HEADS = 8
C_DK = 64
C_DV = 64
C_CHUNK = 64
D_HEADS = 8
D_WIN_H = 8
D_WIN_W = 16

N_EVEN = (DEPTH + 1) // 2
N_ODD = DEPTH // 2

A_QW = A_HEADS * HEAD_DIM
A_KVW = A_KV_HEADS * HEAD_DIM
B_QW = B_HEADS * B_QK
AB_IN = A_QW + 2 * A_KVW + B_QW + B_KV_RANK + B_ROPE
AB_SPLITS = (A_QW, A_QW + A_KVW, A_QW + 2 * A_KVW, A_QW + 2 * A_KVW + B_QW, A_QW + 2 * A_KVW + B_QW + B_KV_RANK)
AB_OUT = A_QW + B_HEADS * B_VDIM

C_W = C_HEADS * C_DK
C_VW = C_HEADS * C_DV
D_W = D_HEADS * HEAD_DIM
CD_IN = 3 * C_W + 2 * C_VW + 3 * D_W
CD_SPLITS = (C_W, 2 * C_W, 3 * C_W, 3 * C_W + C_VW, 3 * C_W + 2 * C_VW, 3 * C_W + 2 * C_VW + D_W, 3 * C_W + 2 * C_VW + 2 * D_W)
CD_OUT = C_VW + D_W

kernel_name = 'hybrid_dit_gqa_mla_hgrn2_natten'


def rms_norm(x, g):
    xf = x.astype(jnp.float32)
    y = xf * lax.rsqrt(jnp.mean(xf * xf, axis=-1, keepdims=True) + EPS)
    return (y * g.astype(jnp.float32)).astype(x.dtype)


def modulate(h, shift, scale):
    return h * (1 + scale[:, None, :]) + shift[:, None, :]


def swiglu(h, w1, w3, w2):
    return (jax.nn.silu(h @ w1) * (h @ w3)) @ w2


def half_ffn(h, g, shift, scale, gate, w1, w3, w2):
    return h + 0.5 * gate[:, None, :] * swiglu(modulate(rms_norm(h, g), shift, scale), w1, w3, w2)


def rope_1d(pos, dim):
    inv = ROPE_THETA ** (-jnp.arange(0, dim, 2, dtype=jnp.float32) / dim)
    ang = pos.astype(jnp.float32)[:, None] * inv[None, :]
    ang = jnp.concatenate([ang, ang], axis=-1)
    return jnp.cos(ang), jnp.sin(ang)


def axial_rope_tables(n_tok, dim):
    t = jnp.arange(n_tok, dtype=jnp.int32)
    cos_r, sin_r = rope_1d(t // GRID_W, dim // 2)
    cos_c, sin_c = rope_1d(t % GRID_W, dim // 2)
    return jnp.concatenate([cos_r, cos_c], axis=-1), jnp.concatenate([sin_r, sin_c], axis=-1)


def _rotate_half(u):
    u1, u2 = jnp.split(u, 2, axis=-1)
    return jnp.concatenate([-u2, u1], axis=-1)


def apply_axial_rope(x, cos, sin):
    h = x.shape[-1] // 2
    rot = jnp.concatenate([_rotate_half(x[..., :h]), _rotate_half(x[..., h:])], axis=-1)
    return x * cos[None, :, None, :].astype(x.dtype) + rot * sin[None, :, None, :].astype(x.dtype)


def blocked_attention(q, k, v, scale):
    b, s, hkv, g, d = q.shape
    nb = s // Q_BLOCK
    qb = jnp.moveaxis(q.reshape(b, nb, Q_BLOCK, hkv, g, d), 1, 0)

    def one_block(qi):
        sc = jnp.einsum('bqhgd,bkhd->bhgqk', qi, k, preferred_element_type=jnp.float32) * scale
        p = jax.nn.softmax(sc, axis=-1).astype(v.dtype)
        return jnp.einsum('bhgqk,bkhd->bqhgd', p, v)

    out = lax.map(one_block, qb)
    return jnp.moveaxis(out, 0, 1).reshape(b, s, hkv * g * v.shape[-1])


def mixer_ab(hx, hc, w_in, a_qn, a_kn, b_qn, b_kn, b_kvn, b_wukv, w_out, with_ctx):
    b, s, _ = hx.shape
    grp = A_HEADS // A_KV_HEADS
    cos_a, sin_a = axial_rope_tables(s, HEAD_DIM)
    cos_b, sin_b = axial_rope_tables(s, B_ROPE)

    def project(h, rotary):
        n = h.shape[1]
        aq, ak, av, bq, bkv, bkr = jnp.split(h @ w_in, AB_SPLITS, axis=-1)
        aq = rms_norm(aq.reshape(b, n, A_HEADS, HEAD_DIM), a_qn)
        ak = rms_norm(ak.reshape(b, n, A_KV_HEADS, HEAD_DIM), a_kn)
        av = av.reshape(b, n, A_KV_HEADS, HEAD_DIM)
        kv = (rms_norm(bkv, b_kvn) @ b_wukv).reshape(b, n, B_HEADS, B_NOPE + B_VDIM)
        bk_rope = jnp.broadcast_to(bkr[:, :, None, :], (b, n, B_HEADS, B_ROPE))
        bq = rms_norm(bq.reshape(b, n, B_HEADS, B_QK), b_qn)
        bk = rms_norm(jnp.concatenate([kv[..., :B_NOPE], bk_rope], axis=-1), b_kn)
        bv = kv[..., B_NOPE:]
        if rotary:
            aq = apply_axial_rope(aq, cos_a, sin_a)
            ak = apply_axial_rope(ak, cos_a, sin_a)
            bq = jnp.concatenate([bq[..., :B_NOPE], apply_axial_rope(bq[..., B_NOPE:], cos_b, sin_b)], axis=-1)
            bk = jnp.concatenate([bk[..., :B_NOPE], apply_axial_rope(bk[..., B_NOPE:], cos_b, sin_b)], axis=-1)
        return aq.reshape(b, n, A_KV_HEADS, grp, HEAD_DIM), ak, av, bq[:, :, :, None, :], bk, bv

    qa, ka, va, qb, kb, vb = project(hx, True)
    qa_c, ka_c, va_c, qb_c, kb_c, vb_c = project(hc, False)
    y_x = jnp.concatenate([
        blocked_attention(qa, jnp.concatenate([ka, ka_c], axis=1), jnp.concatenate([va, va_c], axis=1), HEAD_DIM ** -0.5),
        blocked_attention(qb, jnp.concatenate([kb, kb_c], axis=1), jnp.concatenate([vb, vb_c], axis=1), B_QK ** -0.5),
    ], axis=-1) @ w_out
    if not with_ctx:
        return y_x, None
    y_c = jnp.concatenate([
        blocked_attention(qa_c, ka_c, va_c, HEAD_DIM ** -0.5),
        blocked_attention(qb_c, kb_c, vb_c, B_QK ** -0.5),
    ], axis=-1) @ w_out
    return y_x, y_c


def hgrn_chunk_scan(q, k, v, g, s0, with_output):
    b, h, n, dk = q.shape
    nc = n // C_CHUNK

    def chunks(t):
        return jnp.moveaxis(t.reshape(b, h, nc, C_CHUNK, t.shape[-1]), 2, 0)

    tri = jnp.tril(jnp.ones((C_CHUNK, C_CHUNK), dtype=bool))[:, :, None]

    def step(state, inp):
        qi, ki, vi, gi = inp
        gcum = jnp.cumsum(gi, axis=2)
        glast = gcum[:, :, -1, :]
        new_state = jnp.exp(glast)[..., None] * state + jnp.einsum('bhck,bhcv->bhkv', ki * jnp.exp(glast[:, :, None, :] - gcum), vi)
        if not with_output:
            return new_state, None
        o_inter = jnp.einsum('bhtk,bhkv->bhtv', qi * jnp.exp(gcum), state)
        decay = jnp.exp(jnp.where(tri, gcum[:, :, :, None, :] - gcum[:, :, None, :, :], -jnp.inf))
        attn = jnp.einsum('bhtk,bhsk,bhtsk->bhts', qi, ki, decay)
        return new_state, o_inter + jnp.einsum('bhts,bhsv->bhtv', attn, vi)

    s_fin, o = lax.scan(step, s0, (chunks(q), chunks(k), chunks(v), chunks(g)))
    if with_output:
        o = jnp.moveaxis(o, 0, 2).reshape(b, h, n, v.shape[-1])
    return s_fin, o


def neighbourhood_attention(q, k, v, k_ctx, v_ctx, rpb):
    b, s, h, d = q.shape
    rows = s // GRID_W
    kh = min(D_WIN_H, rows)
    kw = D_WIN_W
    n_win = kh * kw
    cols = np.arange(GRID_W)
    col_idx = np.clip(cols - kw // 2, 0, GRID_W - kw)[:, None] + np.arange(kw)[None, :]
    col_rel = col_idx - cols[:, None] + (D_WIN_W - 1)
    rpb_cols = rpb[:, :, col_rel]
    kg = k.reshape(b, rows, GRID_W, h, d)
    vg = v.reshape(b, rows, GRID_W, h, d)
    q_rows = jnp.moveaxis(q.reshape(b, rows, GRID_W, h, d), 1, 0)
    scale = d ** -0.5

    def one_row(args):
        r, q_row = args
        rs = jnp.clip(r - kh // 2, 0, rows - kh)
        k_win = lax.dynamic_slice_in_dim(kg, rs, kh, axis=1)[:, :, col_idx]
        v_win = lax.dynamic_slice_in_dim(vg, rs, kh, axis=1)[:, :, col_idx]
        row_rel = rs + jnp.arange(kh) - r + (D_WIN_H - 1)
        bias = jnp.transpose(jnp.take(rpb_cols, row_rel, axis=1), (0, 2, 1, 3))
        s_win = jnp.einsum('bchd,brcwhd->bhcrw', q_row, k_win, preferred_element_type=jnp.float32) * scale + bias[None].astype(jnp.float32)
        s_ctx = jnp.einsum('bchd,bnhd->bhcn', q_row, k_ctx, preferred_element_type=jnp.float32) * scale
        p = jax.nn.softmax(jnp.concatenate([s_win.reshape(b, h, GRID_W, n_win), s_ctx], axis=-1), axis=-1).astype(v.dtype)
        return (jnp.einsum('bhcrw,brcwhd->bchd', p[..., :n_win].reshape(b, h, GRID_W, kh, kw), v_win)
                + jnp.einsum('bhcn,bnhd->bchd', p[..., n_win:], v_ctx))

    out = lax.map(one_row, (jnp.arange(rows, dtype=jnp.int32), q_rows))
    return jnp.moveaxis(out, 0, 1).reshape(b, s, h * d)


def mixer_cd(hx, hc, w_in, lb, c_gn, d_qn, d_kn, d_rpb, w_out, with_ctx):
    b, s, _ = hx.shape
    f32 = jnp.float32
    px = jnp.split(hx @ w_in, CD_SPLITS, axis=-1)
    pc = jnp.split(hc @ w_in, CD_SPLITS, axis=-1)

    def heads_first(t, dh):
        return jnp.moveaxis(t.reshape(b, t.shape[1], -1, dh), 2, 1)

    def hgrn_inputs(p, direction):
        z = p[1 + direction].astype(f32)
        lbd = lb[direction]
        logf = jnp.logaddexp(jnp.log(lbd), jnp.log1p(-lbd) + jax.nn.log_sigmoid(z))
        key_in = -jnp.expm1(logf)
        t = (heads_first(jax.nn.silu(p[0].astype(f32)), C_DK), heads_first(key_in, C_DK),
             heads_first(p[3].astype(f32), C_DV), heads_first(logf, C_DK))
        if direction == 1:
            t = tuple(jnp.flip(u, axis=2) for u in t)
        return t

    def hgrn_out(o, p):
        o = jnp.moveaxis(o, 1, 2)
        n = o.shape[1]
        gate = p[4].reshape(b, n, C_HEADS, C_DV)
        return (rms_norm(o, c_gn).astype(gate.dtype) * jax.nn.silu(gate)).reshape(b, n, C_VW)

    s0 = jnp.zeros((b, C_HEADS, C_DK, C_DV), f32)
    qf, kf, vf, gf = hgrn_inputs(pc, 0)
    s_cf, o_cf = hgrn_chunk_scan(qf, kf, vf, gf, s0, with_ctx)
    qf, kf, vf, gf = hgrn_inputs(px, 0)
    _, o_xf = hgrn_chunk_scan(qf, kf, vf, gf, s_cf, True)
    qb_, kb_, vb_, gb_ = hgrn_inputs(pc, 1)
    s_cb, o_cb = hgrn_chunk_scan(qb_, kb_, vb_, gb_, s0, with_ctx)
    qb_, kb_, vb_, gb_ = hgrn_inputs(px, 1)
    _, o_xb = hgrn_chunk_scan(qb_, kb_, vb_, gb_, s_cb, True)
    o_lat = o_xf + jnp.flip(o_xb, axis=2)

    def d_qkv(p):
        n = p[5].shape[1]
        return (rms_norm(p[5].reshape(b, n, D_HEADS, HEAD_DIM), d_qn),
                rms_norm(p[6].reshape(b, n, D_HEADS, HEAD_DIM), d_kn),
                p[7].reshape(b, n, D_HEADS, HEAD_DIM))

    dq, dk, dv = d_qkv(px)
    dq_c, dk_c, dv_c = d_qkv(pc)
    y_x = jnp.concatenate([hgrn_out(o_lat, px), neighbourhood_attention(dq, dk, dv, dk_c, dv_c, d_rpb)], axis=-1) @ w_out
    if not with_ctx:
        return y_x, None
    o_ctx = o_cf + jnp.flip(o_cb, axis=2)
    y_c = jnp.concatenate([hgrn_out(o_ctx, pc), blocked_attention(dq_c[:, :, :, None, :], dk_c, dv_c, HEAD_DIM ** -0.5)], axis=-1) @ w_out
    return y_x, y_c


def setup_inputs(seed: int = 0) -> dict:
    key = jax.random.key(seed)
    ks = jax.random.split(key, 26)

    def nrm(k, shape, scale):
        return jax.random.normal(k, shape, jnp.float32) * scale

    def gain(k, shape):
        return 1.0 + 0.05 * jax.random.normal(k, shape, jnp.float32)

    return {
        'x': nrm(ks[0], (BATCH, SEQ, D_MODEL), 1.0),
        'c': nrm(ks[1], (BATCH, D_MODEL), 1.0),
        'ctx': nrm(ks[2], (BATCH, CTX_LEN, D_MODEL), 1.0),
        'c_ctx': nrm(ks[3], (D_MODEL,), 1.0),
        'norm_g': gain(ks[4], (DEPTH, 3, D_MODEL)),
        'ada_w': nrm(ks[5], (DEPTH, D_MODEL, N_MOD * D_MODEL), ADA_STD),
        'ada_b': nrm(ks[6], (DEPTH, N_MOD * D_MODEL), 0.02),
        'ffn_w1': nrm(ks[7], (DEPTH, 2, D_MODEL, D_FF), D_MODEL ** -0.5),
        'ffn_w3': nrm(ks[8], (DEPTH, 2, D_MODEL, D_FF), D_MODEL ** -0.5),
        'ffn_w2': nrm(ks[9], (DEPTH, 2, D_FF, D_MODEL), D_FF ** -0.5),
        'ab_w_in': nrm(ks[10], (N_EVEN, D_MODEL, AB_IN), D_MODEL ** -0.5),
        'ab_a_qn': gain(ks[11], (N_EVEN, HEAD_DIM)),
        'ab_a_kn': gain(ks[12], (N_EVEN, HEAD_DIM)),
        'ab_b_qn': gain(ks[13], (N_EVEN, B_QK)),
        'ab_b_kn': gain(ks[14], (N_EVEN, B_QK)),
        'ab_b_kvn': gain(ks[15], (N_EVEN, B_KV_RANK)),
        'ab_b_wukv': nrm(ks[16], (N_EVEN, B_KV_RANK, B_HEADS * (B_NOPE + B_VDIM)), B_KV_RANK ** -0.5),
        'ab_w_out': nrm(ks[17], (N_EVEN, AB_OUT, D_MODEL), AB_OUT ** -0.5),
        'cd_w_in': nrm(ks[18], (N_ODD, D_MODEL, CD_IN), D_MODEL ** -0.5),
        'hgrn_lb': nrm(ks[19], (DEPTH, 2, C_W), 0.5),
        'cd_c_gn': gain(ks[20], (N_ODD, C_DV)),
        'cd_d_qn': gain(ks[21], (N_ODD, HEAD_DIM)),
        'cd_d_kn': gain(ks[22], (N_ODD, HEAD_DIM)),
        'cd_d_rpb': nrm(ks[23], (N_ODD, D_HEADS, 2 * D_WIN_H - 1, 2 * D_WIN_W - 1), 0.5),
        'cd_w_out': nrm(ks[24], (N_ODD, CD_OUT, D_MODEL), CD_OUT ** -0.5),
    }


def reference(x, c, ctx, c_ctx, norm_g, ada_w, ada_b, ffn_w1, ffn_w3, ffn_w2, ab_w_in, ab_a_qn, ab_a_kn, ab_b_qn, ab_b_kn, ab_b_kvn, ab_b_wukv, ab_w_out, cd_w_in, hgrn_lb, cd_c_gn, cd_d_qn, cd_d_kn, cd_d_rpb, cd_w_out):
    lb_all = jnp.cumsum(jax.nn.softmax(hgrn_lb.astype(jnp.float32), axis=0), axis=0)
    lb_all = lb_all - lb_all[:1]
    for l in range(DEPTH):
        last = l == DEPTH - 1
        i = l // 2
        mod_x = jnp.split(jax.nn.silu(c) @ ada_w[l] + ada_b[l], N_MOD, axis=-1)
        mod_c = jnp.split(jax.nn.silu(c_ctx)[None, :] @ ada_w[l] + ada_b[l], N_MOD, axis=-1)
        x = half_ffn(x, norm_g[l, 0], mod_x[0], mod_x[1], mod_x[2], ffn_w1[l, 0], ffn_w3[l, 0], ffn_w2[l, 0])
        ctx = half_ffn(ctx, norm_g[l, 0], mod_c[0], mod_c[1], mod_c[2], ffn_w1[l, 0], ffn_w3[l, 0], ffn_w2[l, 0])
        hx = modulate(rms_norm(x, norm_g[l, 1]), mod_x[3], mod_x[4])
        hc = modulate(rms_norm(ctx, norm_g[l, 1]), mod_c[3], mod_c[4])
        if l % 2 == 0:
            y_x, y_c = mixer_ab(hx, hc, ab_w_in[i], ab_a_qn[i], ab_a_kn[i], ab_b_qn[i], ab_b_kn[i], ab_b_kvn[i], ab_b_wukv[i], ab_w_out[i], not last)
        else:
            y_x, y_c = mixer_cd(hx, hc, cd_w_in[i], lb_all[l], cd_c_gn[i], cd_d_qn[i], cd_d_kn[i], cd_d_rpb[i], cd_w_out[i], not last)
        x = x + mod_x[5][:, None, :] * y_x
        x = half_ffn(x, norm_g[l, 2], mod_x[6], mod_x[7], mod_x[8], ffn_w1[l, 1], ffn_w3[l, 1], ffn_w2[l, 1])
        if not last:
            ctx = ctx + mod_c[5][:, None, :] * y_c
            ctx = half_ffn(ctx, norm_g[l, 2], mod_c[6], mod_c[7], mod_c[8], ffn_w1[l, 1], ffn_w3[l, 1], ffn_w2[l, 1])
    return x
```

```python
import numpy as np
import ml_dtypes
import concourse.bass as bass
import concourse.mybir as mybir
from concourse.bass_utils import run_bass_kernel_spmd

F32 = mybir.dt.float32
BF16 = mybir.dt.bfloat16
AF = mybir.ActivationFunctionType
ALU = mybir.AluOpType
AX = mybir.AxisListType

DMA_QUEUES = ("q_sp", "q_act", "q_pool")
COMPUTE = ("pe", "act", "dve", "pool")


def _region(ap):
    name = ap.name
    dims = list(ap.ap)
    off = int(ap.offset)
    sp = str(ap.space)
    if "DRAM" in sp.upper() or "HBM" in sp.upper() or "Dram" in sp:
        lo = off
        hi = off
        for st, n in dims:
            if st >= 0:
                hi += st * (n - 1)
            else:
                lo += st * (n - 1)
        return (name, 0, 1, lo, hi + 1)
    pstep, pn = dims[0]
    es = mybir.dt.size(ap.dtype)
    p0 = off // pstep if pstep > 0 else 0
    f0 = off - p0 * pstep if pstep > 0 else off
    lo = f0
    hi = f0
    for st, n in dims[1:]:
        if st >= 0:
            hi += st * (n - 1)
        else:
            lo += st * (n - 1)
    return (name, p0, p0 + pn, lo * es, (hi + 1) * es)


class Op:
    __slots__ = ("chan", "eng", "fn", "deps", "signal", "ordinal", "idx", "waits", "is_dma")


class Sched:
    EPOCH = {"c": 12000, "d": 3000}

    def __init__(self, nc):
        self.nc = nc
        self.ops = []
        self.chan_count = {}
        self.recs = {}
        self.engobj = {"pe": nc.tensor, "act": nc.scalar, "dve": nc.vector, "pool": nc.gpsimd,
                       "q_sp": nc.sync, "q_act": nc.scalar, "q_pool": nc.gpsimd}
        self.stream = {"pe": "pe", "act": "act", "dve": "dve", "pool": "pool",
                       "q_sp": "sp", "q_act": "act", "q_pool": "pool"}
        self.same_eng_sync = True
        self.sync_same_war = True
        self._names = 0

    def sb(self, name, shape, dtype):
        t = self.nc.alloc_sbuf_tensor(name, list(shape), dtype)
        return t.ap() if hasattr(t, "ap") else t[:]

    def ps(self, name, shape, dtype=F32):
        t = self.nc.alloc_psum_tensor(name, list(shape), dtype)
        return t.ap() if hasattr(t, "ap") else t[:]

    DMA_SLOTS = 8

    def op(self, chan, fn, outs=(), ins=(), nosync_same=False):
        o = Op()
        o.fn = fn
        o.is_dma = chan.startswith("q_")
        deps = {}
        if o.is_dma:
            qn = self.chan_count.get(("n", chan), 0)
            self.chan_count[("n", chan)] = qn + 1
            base = chan
            chan = f"{base}#{qn % self.DMA_SLOTS}"
            self.engobj[chan] = self.engobj[base]
            self.stream[chan] = self.stream[base]
        o.chan = chan
        seq = self.chan_count.get(chan, 0) + 1
        self.chan_count[chan] = seq
        o.ordinal = seq
        if o.is_dma and seq > 1:
            deps[chan] = seq - 1
        my_stream = self.stream[chan]

        def add_dep(c, s, kind):
            if c == chan and not o.is_dma:
                if chan == "pe" or nosync_same or not self.same_eng_sync:
                    return
                if kind == "WAR" and not self.sync_same_war:
                    return
            if deps.get(c, 0) < s:
                deps[c] = s

        for ap, is_w in [(a, False) for a in ins] + [(a, True) for a in outs]:
            if ap is None:
                continue
            name, p0, p1, f0, f1 = _region(ap)
            lst = self.recs.setdefault(name, [])
            keep = []
            for r in lst:
                ov = not (r[1] <= p0 or p1 <= r[0] or r[3] <= f0 or f1 <= r[2])
                if ov and r[4] == chan and r[5] == seq:
                    ov = False
                    if is_w and (p0 <= r[0] and r[1] <= p1 and f0 <= r[2] and r[3] <= f1):
                        continue
                if ov:
                    if is_w or r[6]:
                        kind = "RAW" if (r[6] and not is_w) else ("WAW" if (r[6] and is_w) else "WAR")
                        add_dep(r[4], r[5], kind)
                    covered = (p0 <= r[0] and r[1] <= p1 and f0 <= r[2] and r[3] <= f1)
                    if is_w and covered:
                        continue
                    if (not is_w) and (not r[6]) and covered and r[4] == chan:
                        continue
                keep.append(r)
            keep.append([p0, p1, f0, f1, chan, seq, is_w])
            self.recs[name] = keep
        o.deps = deps
        o.signal = o.is_dma
        o.idx = len(self.ops)
        self.ops.append(o)
        return o

    def dma(self, out, in_, q="q_sp", **kw):
        e = self.engobj[q]
        return self.op(q, lambda: e.dma_start(out=out, in_=in_, **kw), outs=[out], ins=[in_])

    def matmul(self, out, lhsT, rhs, start=True, stop=True, **kw):
        return self.op("pe", lambda: self.nc.tensor.matmul(out, lhsT, rhs, start=start, stop=stop, **kw),
                       outs=[out], ins=[lhsT, rhs])

    def transpose(self, out, in_, ident):
        return self.op("pe", lambda: self.nc.tensor.transpose(out, in_, ident), outs=[out], ins=[in_, ident])

    def act(self, out, in_, func, scale=1.0, bias=None, accum_out=None):
        ins = [in_]
        if bias is not None and not isinstance(bias, (int, float)):
            ins.append(bias)
        if not isinstance(scale, (int, float)):
            ins.append(scale)
        outs = [out] + ([accum_out] if accum_out is not None else [])
        kw = {}
        if bias is not None:
            kw["bias"] = bias
        if accum_out is not None:
            kw["accum_out"] = accum_out
        return self.op("act", lambda: self.nc.scalar.activation(out=out, in_=in_, func=func, scale=scale, **kw),
                       outs=outs, ins=ins)

    def tt(self, out, in0, in1, op, eng="dve"):
        e = self.engobj[eng]
        return self.op(eng, lambda: e.tensor_tensor(out=out, in0=in0, in1=in1, op=op), outs=[out], ins=[in0, in1])

    def ts(self, out, in0, s1, op0, s2=None, op1=None, eng="dve"):
        e = self.engobj[eng]
        ins = [in0] + [s for s in (s1, s2) if s is not None and not isinstance(s, (int, float))]
        kw = dict(op0=op0)
        if op1 is not None:
            kw["op1"] = op1
        return self.op(eng, lambda: e.tensor_scalar(out=out, in0=in0, scalar1=s1, scalar2=s2, **kw),
                       outs=[out], ins=ins)

    def stt(self, out, in0, scalar, in1, op0, op1):
        ins = [in0, in1] + ([scalar] if not isinstance(scalar, (int, float)) else [])
        return self.op("dve", lambda: self.nc.vector.scalar_tensor_tensor(out=out, in0=in0, scalar=scalar, in1=in1,
                                                                         op0=op0, op1=op1), outs=[out], ins=ins)

    def copy(self, out, in_, eng="dve"):
        e = self.engobj[eng]
        if eng == "act":
            return self.op("act", lambda: e.copy(out=out, in_=in_), outs=[out], ins=[in_])
        return self.op(eng, lambda: e.tensor_copy(out=out, in_=in_), outs=[out], ins=[in_])

    def memset(self, ap, val, eng="dve"):
        e = self.engobj[eng]
        return self.op(eng, lambda: e.memset(ap, val), outs=[ap])

    def finalize(self, final_wait_chans=("q_pool", "q_sp", "q_act")):
        ops = self.ops
        wm = {}
        by_chan = {}
        for o in ops:
            by_chan.setdefault(o.chan, []).append(o)
        for o in ops:
            st = self.stream[o.chan]
            w = wm.setdefault(st, {})
            o.waits = []
            for c, s in o.deps.items():
                if w.get(c, 0) >= s:
                    continue
                w[c] = s
                o.waits.append((c, s))
                by_chan[c][s - 1].signal = True
        last = {}
        for c in by_chan:
            if c.startswith("q_"):
                by_chan[c][-1].signal = True
                last[c] = by_chan[c][-1].ordinal
        sig_ord = {}
        for c, lst in by_chan.items():
            n = 0
            for o in lst:
                if o.signal:
                    n += 1
                    sig_ord[(c, o.ordinal)] = n
        sems = {}

        def sem_for(c, n):
            kind = "d" if c.startswith("q_") else "c"
            E = self.EPOCH[kind]
            k = (n - 1) // E
            v = (n - 1) % E + 1
            key = (c, k)
            if key not in sems:
                sems[key] = self.nc.alloc_semaphore(f"s_{c}_{k}")
            return sems[key], (v * 16 if kind == "d" else v)

        nwait = 0
        for o in ops:
            eng = self.engobj[o.chan]
            for c, s in o.waits:
                sem, val = sem_for(c, sig_ord[(c, s)])
                eng.wait_ge(sem, val)
                nwait += 1
            ins = o.fn()
            if o.signal:
                sem, val = sem_for(o.chan, sig_ord[(o.chan, o.ordinal)])
                ins.then_inc(sem, 16 if o.is_dma else 1)
        for c, n in last.items():
            sem, val = sem_for(c, sig_ord[(c, n)])
            self.nc.sync.wait_ge(sem, val)
        self.stats = dict(n_ops=len(ops), n_waits=nwait, n_sems=len(sems),
                          per_chan={c: len(l) for c, l in by_chan.items()})
        return self.stats


D = 1024
DFF = 2816
NTX = 2048
NCTX = 256
NT = NTX + NCTX
EPS = 1e-6
GROUPS = [(0, 512), (512, 512), (1024, 512), (1536, 512), (2048, 256)]
FF_BLOCKS = [(0, 3), (3, 3), (6, 3), (9, 3), (12, 3), (15, 3), (18, 1), (19, 3)]


class Ctx:
    def __init__(self, S, consts):
        self.S = S
        nc = S.nc
        self.nc = nc
        self.ident_f = S.sb("ident_f", [128, 128], F32)
        self.ident_b = S.sb("ident_b", [128, 128], BF16)
        self.ones_b = S.sb("ones_b", [128, 128], BF16)
        self.blk64_b = S.sb("blk64_b", [128, 128], BF16)
        S.dma(self.ident_f, consts["ident"])
        S.dma(self.ident_b, consts["ident"], q="q_pool")
        S.dma(self.ones_b, consts["ones"], q="q_pool")
        S.dma(self.blk64_b, consts["blk64"], q="q_pool")
        self.eps_t = S.sb("eps_t", [128, 1], F32)
        S.memset(self.eps_t, EPS)
        self.one_t = S.sb("one_t", [128, 1], F32)
        S.memset(self.one_t, 1.0)
        self.xT = S.sb("xT", [128, 8, NT], F32)
        self.hT = S.sb("hT", [128, 8, NT], BF16)
        self.pbank = [S.ps(f"pb{i}", [128, 512], F32) for i in range(8)]
        self._rr = {}
        self.ARENA = 23990
        self.arena = S.sb("arena", [128, self.ARENA], F32)
        self._aoff = 0

    def phase(self, name):
        self._aoff = 0
        for a in list(self.__dict__):
            if a.startswith("t_"):
                delattr(self, a)

    def tmp(self, shape, dtype):
        n = int(np.prod(shape[1:]))
        es = mybir.dt.size(dtype)
        nb = (n * es + 63) // 64 * 64
        o = self._aoff
        assert o + nb <= self.ARENA * 4, f"arena overflow {o + nb} > {self.ARENA * 4}"
        self._aoff = o + nb
        v = self.arena[:, o // 4:(o + nb) // 4]
        if dtype != F32:
            v = v.bitcast(dtype)
        v = v[:, 0:n]
        if shape[0] < 128:
            v = v[0:shape[0], :]
        if len(shape) == 3:
            v = v.rearrange("p (a b) -> p a b", a=shape[1])
        elif len(shape) == 4:
            v = v.rearrange("p (a b c) -> p a b c", a=shape[1], b=shape[2])
        return v

    def rr(self, key, n):
        i = self._rr.get(key, 0)
        self._rr[key] = i + 1
        return i % n


def load_xT(C, x_dram, col0, ntok):
    S = C.S
    if not hasattr(C, "t_xin"):
        C.t_xin = [C.tmp([128, 1024], F32) for i in range(2)]
    for t in range(ntok // 128):
        buf = C.t_xin[C.rr("xin", 2)]
        S.dma(buf, x_dram[t * 128:(t + 1) * 128, :])
        for half in range(2):
            pb = C.pbank[C.rr("ldps", 2)]
            for cc in range(4):
                c = half * 4 + cc
                S.transpose(pb[:, cc * 128:(cc + 1) * 128], buf[:, c * 128:(c + 1) * 128], C.ident_f)
            dst = C.xT[:, half * 4:(half + 1) * 4, col0 + t * 128: col0 + (t + 1) * 128]
            src = pb.rearrange("p (c t) -> p c t", c=4)
            if half == 0:
                S.copy(dst, src, eng="dve")
            else:
                S.copy(dst, src, eng="act")


def store_xT(C, out_dram, col0, ntok):
    S = C.S
    if not hasattr(C, "t_xout"):
        C.t_xout = [C.tmp([128, 1024], F32) for i in range(2)]
    for t in range(ntok // 128):
        buf = C.t_xout[C.rr("xout", 2)]
        for half in range(2):
            pb = C.pbank[C.rr("ldps", 2)]
            for cc in range(4):
                c = half * 4 + cc
                S.transpose(pb[:, cc * 128:(cc + 1) * 128],
                            C.xT[:, c, col0 + t * 128: col0 + (t + 1) * 128], C.ident_f)
            if half == 0:
                S.copy(buf[:, 0:512], pb, eng="dve")
            else:
                S.copy(buf[:, 512:1024], pb, eng="act")
        S.dma(out_dram[t * 128:(t + 1) * 128, :], buf, q="q_pool")


def adaln(C, l, c_dram, cctx_dram, ada_w, ada_b, norm_g):
    S = C.S
    if not hasattr(C, "t_ada_wb"):
        C.t_ada_wb = [C.tmp([128, 8, 512], BF16) for i in range(2)]
    if not hasattr(C, "vstage"):
        C.vstage = S.sb("vstage", [128, 128], F32)
        C.vecs = S.sb("vecs", [128, 128], F32)
        C.sc = S.sb("sc", [128, 8, 2], BF16)
        S.memset(C.vstage, 0.0)
    if not hasattr(C, "modT"):
        C.modT = S.sb("modT", [128, 9, 8, 2], F32)
        C.gm = S.sb("gm", [128, 3, 8, 2], F32)
        C.sh = S.sb("sh", [128, 3, 8, 2], F32)
        C.gt = S.sb("gt", [128, 3, 8, 2], F32)
    S.dma(C.vstage[0:8, :], c_dram.rearrange("o (c p) -> (o c) p", p=128))
    S.dma(C.vstage[8:16, :], cctx_dram.rearrange("(c p) -> c p", p=128))
    S.dma(C.vstage[16:88, :], ada_b[l].rearrange("(r p) -> r p", p=128))
    S.dma(C.vstage[88:112, :], norm_g[l].rearrange("k (c p) -> (k c) p", p=128))
    pb = C.pbank[7]
    S.transpose(pb[:, 0:128], C.vstage, C.ident_f)
    S.copy(C.vecs, pb[:, 0:128])
    adab = C.vecs[:, 16:88].rearrange("p (m c) -> p m c", m=9)
    ng = C.vecs[:, 88:112].rearrange("p (k c) -> p k c", k=3)
    S.act(C.sc, C.vecs[:, 0:16].rearrange("p (w c) -> p c w", w=2), AF.Silu)
    pbv = pb[:, 128:272].rearrange("p (m c w) -> p m c w", m=9, c=8)
    for mh in range(18):
        m, hf = mh // 2, mh % 2
        wb = C.t_ada_wb[C.rr("adaw", 2)]
        S.dma(wb, ada_w[l][:, mh * 512:(mh + 1) * 512].rearrange("(c p) n -> p c n", p=128), q="q_pool")
        for o4 in range(4):
            oc = hf * 4 + o4
            for c in range(8):
                S.matmul(pbv[:, m, oc, :], wb[:, c, o4 * 128:(o4 + 1) * 128], C.sc[:, c, :],
                         start=(c == 0), stop=(c == 7))
    for w in range(2):
        S.tt(C.modT[:, :, :, w], pbv[:, :, :, w], adab, ALU.add)
    for k in range(3):
        for w in range(2):
            S.stt(C.gm[:, k, :, w], C.modT[:, 3 * k + 1, :, w], 1.0, ng[:, k, :], ALU.add, ALU.mult)
        S.copy(C.sh[:, k], C.modT[:, 3 * k])
        S.ts(C.gt[:, k], C.modT[:, 3 * k + 2], 0.5 if k != 1 else 1.0, ALU.mult)


def norm_mod(C, k, g0, gn):
    S = C.S
    if not hasattr(C, "t_sq"):
        C.t_sq = [C.tmp([128, 512], BF16) for i in range(3)]
        C.t_rstd = [C.tmp([128, 512], F32) for i in range(2)]
        C.t_lnt = [C.tmp([128, 512], F32) for i in range(2)]
        C.t_xn = [C.tmp([128, 512], F32) for i in range(3)]
    pb = C.pbank[6]
    for c in range(8):
        sq = C.t_sq[C.rr("sq", 3)]
        S.act(sq[:, :gn], C.xT[:, c, g0:g0 + gn], AF.Square)
        S.matmul(pb[:, :gn], C.ones_b, sq[:, :gn], start=(c == 0), stop=(c == 7))
    i = C.rr("rstd", 2)
    lnt, rstd = C.t_lnt[i], C.t_rstd[i]
    S.act(lnt[:, :gn], pb[:, :gn], AF.Ln, scale=1.0 / D, bias=C.eps_t)
    S.act(rstd[:, :gn], lnt[:, :gn], AF.Exp, scale=-0.5)
    w = 1 if g0 >= NTX else 0
    for c in range(8):
        xn = C.t_xn[C.rr("xn", 3)]
        S.tt(xn[:, :gn], C.xT[:, c, g0:g0 + gn], rstd[:, :gn], ALU.mult)
        if c % 2 == 0:
            S.act(C.hT[:, c, g0:g0 + gn], xn[:, :gn], AF.Identity,
                  scale=C.gm[:, k, c, w:w + 1], bias=C.sh[:, k, c, w:w + 1])
        else:
            S.ts(C.hT[:, c, g0:g0 + gn], xn[:, :gn], C.gm[:, k, c, w:w + 1], ALU.mult,
                 C.sh[:, k, c, w:w + 1], ALU.add, eng="pool")


def ffn(C, k, w1, w3, w2):
    S = C.S
    if not hasattr(C, "t_w1b"):
        C.t_w1b = [C.tmp([128, 8, 384], BF16) for i in range(2)]
        C.t_w3b = [C.tmp([128, 8, 384], BF16) for i in range(2)]
        C.t_w2b = [C.tmp([128, 3, 1024], BF16) for i in range(2)]
        C.t_gT = [C.tmp([128, 3, 512], BF16) for i in range(2)]
        C.t_sil = [C.tmp([128, 512], F32) for i in range(2)]
    for (g0, gn) in GROUPS:
        norm_mod(C, k, g0, gn)
    pending = None

    def down(item):
        wi, j0, nb, g0, gn, gT = item
        w2b = C.t_w2b[wi]
        wsel = 1 if g0 >= NTX else 0
        for oc in range(8):
            pb = C.pbank[4 + C.rr("ypb", 2)]
            for jj in range(nb):
                S.matmul(pb[:, :gn], w2b[:, jj, oc * 128:(oc + 1) * 128], gT[:, jj, :gn],
                         start=(jj == 0), stop=(jj == nb - 1))
            S.stt(C.xT[:, oc, g0:g0 + gn], pb[:, :gn], C.gt[:, k, oc, wsel:wsel + 1],
                  C.xT[:, oc, g0:g0 + gn], ALU.mult, ALU.add)

    for bi, (j0, nb) in enumerate(FF_BLOCKS):
        wi = bi % 2
        w1b, w3b, w2b = C.t_w1b[wi], C.t_w3b[wi], C.t_w2b[wi]
        S.dma(w1b[:, :, :nb * 128], w1[:, j0 * 128:(j0 + nb) * 128].rearrange("(c p) n -> p c n", p=128), q="q_pool")
        S.dma(w3b[:, :, :nb * 128], w3[:, j0 * 128:(j0 + nb) * 128].rearrange("(c p) n -> p c n", p=128), q="q_pool")
        S.dma(w2b[:, :nb, :], w2[j0 * 128:(j0 + nb) * 128, :].rearrange("(j p) n -> p j n", p=128), q="q_pool")
        for (g0, gn) in GROUPS:
            gT = C.t_gT[C.rr("gT", 2)]
            for jj in range(nb):
                pa = C.pbank[0 + C.rr("apb", 2)]
                pbb = C.pbank[2 + C.rr("bpb", 2)]
                for c in range(8):
                    S.matmul(pa[:, :gn], w1b[:, c, jj * 128:(jj + 1) * 128], C.hT[:, c, g0:g0 + gn],
                             start=(c == 0), stop=(c == 7))
                for c in range(8):
                    S.matmul(pbb[:, :gn], w3b[:, c, jj * 128:(jj + 1) * 128], C.hT[:, c, g0:g0 + gn],
                             start=(c == 0), stop=(c == 7))
                sil = C.t_sil[C.rr("sil", 2)]
                S.act(sil[:, :gn], pa[:, :gn], AF.Silu)
                S.tt(gT[:, jj, :gn], sil[:, :gn], pbb[:, :gn], ALU.mult)
            if pending is not None:
                down(pending)
            pending = (wi, j0, nb, g0, gn, gT)
    down(pending)


def load_mods(C, l, mods_dram, ng_dram):
    S = C.S
    if not hasattr(C, "ngt"):
        C.ngt = S.sb("ngt", [128, 3, 8], F32)
    if not hasattr(C, "modT"):
        C.modT = S.sb("modT", [128, 9, 8, 2], F32)
        C.gm = S.sb("gm", [128, 3, 8, 2], F32)
        C.sh = S.sb("sh", [128, 3, 8, 2], F32)
        C.gt = S.sb("gt", [128, 3, 8, 2], F32)
    S.dma(C.modT, mods_dram[:, l])
    S.dma(C.ngt, ng_dram[:, l])
    for k in range(3):
        for w in range(2):
            S.stt(C.gm[:, k, :, w], C.modT[:, 3 * k + 1, :, w], 1.0, C.ngt[:, k, :], ALU.add, ALU.mult)
        S.copy(C.sh[:, k], C.modT[:, 3 * k])
        S.ts(C.gt[:, k], C.modT[:, 3 * k + 2], 0.5 if k != 1 else 1.0, ALU.mult)


def load_xT_fm(C, xT_dram):
    for c in range(8):
        C.S.dma(C.xT[:, c, :], xT_dram[c], q="q_sp" if c % 2 == 0 else "q_pool")


def store_xT_fm(C, xT_dram):
    for c in range(8):
        C.S.dma(xT_dram[c], C.xT[:, c, :], q="q_pool")


def normrope_unit(C, wplain, wrot, R, nrm_blk, n_norm, gcol, tab_cos, tab_sin, out_dram, extra=None):
    S = C.S
    if not hasattr(C, "t_wch"):
        C.t_wch = [C.tmp([128, 8, 128], BF16) for _ in range(4)]
        C.t_nsq = [C.tmp([128, 512], BF16) for _ in range(2)]
        C.t_nln = [C.tmp([128, 512], F32) for _ in range(2)]
        C.t_nrs = [C.tmp([128, 512], F32) for _ in range(2)]
        C.t_t1 = [C.tmp([128, 512], F32) for _ in range(2)]
        C.t_t2 = [C.tmp([128, 512], F32) for _ in range(2)]
        C.t_ob = [C.tmp([128, 512], BF16) for _ in range(3)]
    wp = C.t_wch[C.rr("wch", 4)]
    wr = C.t_wch[C.rr("wch", 4)]
    S.dma(wp, wplain.rearrange("(c p) n -> p c n", p=128), q="q_pool")
    S.dma(wr, wrot.rearrange("(c p) n -> p c n", p=128), q="q_pool")
    g_ap, gp_ap = gcol
    for (g0, gn) in GROUPS:
        pa = C.pbank[0 + C.rr("npa", 2)]
        pr = C.pbank[2 + C.rr("npr", 2)]
        nex = 0 if extra is None else sum(e[2] for e in extra)
        for dst, wsb in ((pa, wp), (pr, wr)):
            k = 0
            tot = 8 + nex
            for c in range(8):
                S.matmul(dst[:R, :gn], wsb[:, c, :R], C.hT[:, c, g0:g0 + gn], start=(k == 0), stop=(k == tot - 1))
                k += 1
            if extra is not None:
                for (w_sb, rhs_fn, nch) in extra:
                    for c in range(nch):
                        S.matmul(dst[:R, :gn], w_sb[:, c, :R], rhs_fn(g0, gn, c), start=(k == 0), stop=(k == tot - 1))
                        k += 1
        i = C.rr("nsq", 2)
        sq, lnt, rs, t1, t2 = C.t_nsq[i], C.t_nln[i], C.t_nrs[i], C.t_t1[i], C.t_t2[i]
        S.act(sq[:R, :gn], pa[:R, :gn], AF.Square)
        ps = C.pbank[4 + C.rr("nps", 2)]
        S.matmul(ps[:R, :gn], nrm_blk[:R, :R], sq[:R, :gn])
        S.act(lnt[:R, :gn], ps[:R, :gn], AF.Sqrt, scale=1.0 / n_norm, bias=C.eps_t[:R])
        S.op("dve", lambda rs=rs, lnt=lnt, gn=gn: C.nc.vector.reciprocal(out=rs[:R, :gn], in_=lnt[:R, :gn]),
             outs=[rs[:R, :gn]], ins=[lnt[:R, :gn]])
        S.stt(t1[:R, :gn], pa[:R, :gn], g_ap[:R], tab_cos[:R, g0:g0 + gn], ALU.mult, ALU.mult)
        S.stt(t2[:R, :gn], pr[:R, :gn], gp_ap[:R], tab_sin[:R, g0:g0 + gn], ALU.mult, ALU.mult)
        S.tt(t1[:R, :gn], t1[:R, :gn], t2[:R, :gn], ALU.add, eng="pool")
        ob = C.t_ob[C.rr("nob", 3)]
        S.tt(ob[:R, :gn], t1[:R, :gn], rs[:R, :gn], ALU.mult)
        S.dma(out_dram[:R, g0:g0 + gn], ob[:R, :gn], q="q_sp")


def ab_front(C, W, T, O):
    S = C.S
    for (g0, gn) in GROUPS:
        norm_mod(C, 1, g0, gn)
    gains = C.tmp([128, 16], F32)
    S.dma(gains, W["gains"])
    cosA = C.tmp([128, NT], F32); sinA = C.tmp([128, NT], F32)
    cosB, sinB = cosA, sinA
    S.dma(cosA, T["cosA"]); S.dma(sinA, T["sinA"])
    ckvn = C.tmp([128, 2, NT], BF16)
    ch = W["ab_chunks"]
    wkv = [C.tmp([128, 8, 128], BF16) for _ in range(2)]
    for j in range(2):
        S.dma(wkv[j], ch[26 + j].rearrange("(c p) n -> p c n", p=128), q="q_pool")
    sqk = [C.tmp([128, 512], BF16) for _ in range(2)]
    lnk = C.tmp([128, 512], F32); rsk = C.tmp([128, 512], F32); tk = C.tmp([128, 512], F32)
    for (g0, gn) in GROUPS:
        pk = [C.pbank[0], C.pbank[1]]
        for j in range(2):
            for c in range(8):
                S.matmul(pk[j][:, :gn], wkv[j][:, c, :], C.hT[:, c, g0:g0 + gn], start=(c == 0), stop=(c == 7))
            S.act(sqk[j][:, :gn], pk[j][:, :gn], AF.Square)
        ps = C.pbank[4]
        for j in range(2):
            S.matmul(ps[:, :gn], C.ones_b, sqk[j][:, :gn], start=(j == 0), stop=(j == 1))
        S.act(lnk[:, :gn], ps[:, :gn], AF.Ln, scale=1.0 / 256, bias=C.eps_t)
        S.act(rsk[:, :gn], lnk[:, :gn], AF.Exp, scale=-0.5)
        for j in range(2):
            S.tt(tk[:, :gn], pk[j][:, :gn], rsk[:, :gn], ALU.mult)
            S.ts(ckvn[:, j, g0:g0 + gn], tk[:, :gn], gains[:, 8 + j:9 + j], ALU.mult)
    for c4 in range(4):
        normrope_unit(C, ch[c4], ch[4 + c4], 128, C.blk64_b, 64, (gains[:, 0:1], gains[:, 1:2]), cosA, sinA, O["qa"][c4])
    normrope_unit(C, ch[8], ch[9], 128, C.blk64_b, 64, (gains[:, 2:3], gains[:, 3:4]), cosA, sinA, O["ka"])
    S.dma(cosB, T["cosB"]); S.dma(sinB, T["sinB"])
    for h in range(8):
        normrope_unit(C, ch[10 + h], ch[18 + h], 96, C.ones_b, 96, (gains[:, 4:5], gains[:, 5:6]), cosB, sinB, O["qb"][h])
    wuk = [C.tmp([128, 2, 128], BF16) for _ in range(2)]
    for h in range(8):
        wk = wuk[h % 2]
        S.dma(wk, W["wukv_k"][h].rearrange("(c p) n -> p c n", p=128), q="q_pool")
        extra = [(wk, (lambda g0, gn, c: ckvn[:, c, g0:g0 + gn]), 2)]
        normrope_unit(C, ch[28], ch[29], 96, C.ones_b, 96, (gains[:, 6:7], gains[:, 7:8]), cosB, sinB, O["kb"][h],
                      extra=extra)
    wav = C.tmp([128, 8, 128], BF16)
    S.dma(wav, ch[30].rearrange("(c p) n -> p c n", p=128), q="q_pool")
    wbv = C.tmp([128, 2, 512], BF16)
    S.dma(wbv, W["wukv_v"].rearrange("(c p) n -> p c n", p=128), q="q_pool")
    vat = [C.tmp([128, 2, 65], BF16) for _ in range(2)]
    vbt = [C.tmp([128, 8, 65], BF16) for _ in range(2)]
    for i in range(2):
        S.memset(vat[i], 1.0)
        S.memset(vbt[i], 1.0)
    for t in range(NT // 128):
        pa = C.pbank[0 + C.rr("vpa", 2)]
        pbv = C.pbank[2 + C.rr("vpb", 2)]
        for c in range(8):
            S.matmul(pa[:, 0:128], C.hT[:, c, t * 128:(t + 1) * 128], wav[:, c, :], start=(c == 0), stop=(c == 7))
        for c in range(2):
            S.matmul(pbv[:, 0:512], ckvn[:, c, t * 128:(t + 1) * 128], wbv[:, c, :], start=(c == 0), stop=(c == 1))
        va = vat[t % 2]; vb = vbt[t % 2]
        S.copy(va[:, :, 0:64], pa[:, 0:128].rearrange("p (h d) -> p h d", h=2), eng="act")
        S.copy(vb[:, :, 0:64], pbv[:, 0:512].rearrange("p (h d) -> p h d", h=8), eng="dve")
        S.dma(O["va"][t * 128:(t + 1) * 128], va, q="q_sp")
        S.dma(O["vb"][t * 128:(t + 1) * 128], vb, q="q_sp")


NK = 16640
KT_PER_BLK = 26
N_KBLK = 5


def attention(C, A):
    S = C.S
    ones_f = C.tmp([128, 64], F32)
    S.memset(ones_f, 1.0)
    qh = [C.tmp([96, NT], BF16) for _ in range(2)]
    kblk = [C.tmp([96, KT_PER_BLK * 128], BF16) for _ in range(2)]
    vblk = [C.tmp([128, KT_PER_BLK, 65], BF16) for _ in range(2)]
    pT = [C.tmp([128, 512], BF16) for _ in range(3)]
    osb = [C.tmp([64, 512], F32) for _ in range(2)]
    rr_ = [C.tmp([128, 512], F32) for _ in range(2)]
    yb = [C.tmp([64, 512], BF16) for _ in range(2)]
    acc = [C.pbank[i] for i in range(5)]
    sps = [C.pbank[5], C.pbank[6]]
    pbc = C.pbank[7]
    heads = [("a", h) for h in range(8)] + [("b", h) for h in range(8)]
    for hi, (kind, h) in enumerate(heads):
        q = qh[hi % 2]
        if kind == "a":
            dk, scale = 64, 64 ** -0.5
            S.dma(q[:64, :], A["qa"][h // 2][(h % 2) * 64:(h % 2) * 64 + 64, :], q="q_sp")
            g = h // 4
            ksrc = A["ka"][g * 64:(g + 1) * 64, :]
            vsrc = A["va"][:, g, :]
        else:
            dk, scale = 96, 96 ** -0.5
            S.dma(q[:96, :], A["qb"][h], q="q_sp")
            ksrc = A["kb"][h]
            vsrc = A["vb"][:, h, :]
        for b in range(N_KBLK):
            i = C.rr("kblk", 2)
            kb_, vb_ = kblk[i], vblk[i]
            k0 = b * KT_PER_BLK * 128
            S.dma(kb_[:dk, :], ksrc[:, k0:k0 + KT_PER_BLK * 128], q="q_sp")
            S.dma(vb_, vsrc[k0:k0 + KT_PER_BLK * 128, :].rearrange("(t p) e -> p t e", p=128), q="q_pool")
            for gi, (g0, gn) in enumerate(GROUPS):
                if gi == 4:
                    if b != N_KBLK - 1:
                        continue
                    tiles = [KT_PER_BLK - 2, KT_PER_BLK - 1]
                    first_t, last_t = tiles[0], tiles[-1]
                    is_first = lambda t: t == first_t
                    is_last = lambda t: t == last_t
                else:
                    tiles = list(range(KT_PER_BLK))
                    is_first = lambda t, b=b: (b == 0 and t == 0)
                    is_last = lambda t, b=b: (b == N_KBLK - 1 and t == KT_PER_BLK - 1)
                for t in tiles:
                    sp = sps[C.rr("sps", 2)]
                    S.matmul(sp[:, :gn], kb_[:dk, t * 128:(t + 1) * 128], q[:dk, g0:g0 + gn])
                    p = pT[C.rr("pT", 3)]
                    S.act(p[:, :gn], sp[:, :gn], AF.Exp, scale=scale)
                    S.matmul(acc[gi][:65, :gn], vb_[:, t, :], p[:, :gn], start=is_first(t), stop=is_last(t))
        for gi, (g0, gn) in enumerate(GROUPS):
            j = C.rr("osb", 2)
            o, r, y = osb[j], rr_[j], yb[j]
            S.copy(o[:, :gn], acc[gi][0:64, :gn], eng="act")
            S.op("dve", lambda r=r, gi=gi, gn=gn: C.nc.vector.reciprocal(out=r[64:65, :gn], in_=acc[gi][64:65, :gn]),
                 outs=[r[64:65, :gn]], ins=[acc[gi][64:65, :gn]])
            S.matmul(pbc[0:64, :gn], ones_f[64:65, 0:64], r[64:65, :gn])
            S.tt(y[:, :gn], o[:, :gn], pbc[0:64, :gn], ALU.mult)
            row0 = (0 if kind == "a" else 512) + h * 64
            S.dma(A["yT"][row0:row0 + 64, g0:g0 + gn], y[:, :gn], q="q_sp")


def out_proj(C, yT_dram, wout_dram):
    S = C.S
    wo = C.tmp([128, 8, 1024], BF16)
    S.dma(wo, wout_dram.rearrange("(c p) n -> p c n", p=128), q="q_pool")
    yT = C.tmp([128, 8, NT], BF16)
    for c in range(8):
        S.dma(yT[:, c, :], yT_dram[c * 128:(c + 1) * 128, :], q="q_sp")
    for (g0, gn) in GROUPS:
        wsel = 1 if g0 >= NTX else 0
        for oc in range(8):
            pb = C.pbank[C.rr("opb", 2)]
            for c in range(8):
                S.matmul(pb[:, :gn], wo[:, c, oc * 128:(oc + 1) * 128], yT[:, c, g0:g0 + gn], start=(c == 0), stop=(c == 7))
            S.stt(C.xT[:, oc, g0:g0 + gn], pb[:, :gn], C.gt[:, 1, oc, wsel:wsel + 1],
                  C.xT[:, oc, g0:g0 + gn], ALU.mult, ALU.add)


NCH_X = NTX // 64
NCH_C = NCTX // 64
NSLOT = 2 + NCH_C + NCH_X
XINIT = 1 + NCH_C


def slot_of(direction, c):
    if direction == 0:
        return (1 + (c - NCH_X)) if c >= NCH_X else (XINIT + 1 + c)
    return (1 + (NCH_X + NCH_C - 1 - c)) if c >= NCH_X else (XINIT + 1 + (NCH_X - 1 - c))


def hgrn(C, W, X, final):
    S = C.S
    nc = C.nc
    T = lambda dt=F32: C.tmp([128, 512], dt)
    hl = C.tmp([128, 16], F32); S.dma(hl, W["hl"])
    lbT = C.tmp([128, 8], F32); lbe = C.tmp([128, 8], F32)
    S.tt(lbe, hl[:, 8:16], hl[:, 0:8], ALU.subtract)
    S.act(lbe, lbe, AF.Exp, scale=-1.0)
    S.ts(lbe, lbe, 1.0, ALU.add)
    S.op("dve", lambda: nc.vector.reciprocal(out=lbT, in_=lbe), outs=[lbT], ins=[lbe])
    m01 = T(); S.dma(m01, W["m01"])
    Mf = C.tmp([128, 128], F32); Mb = C.tmp([128, 128], F32)
    S.dma(Mf, W["Mf"]); S.dma(Mb, W["Mb"])
    cgn = C.tmp([64, 1], F32); S.dma(cgn, W["cgn"])
    ones64 = C.tmp([64, 64], BF16); S.memset(ones64, 1.0)
    wch = [C.tmp([128, 8, 128], BF16) for _ in range(3)]
    wv = C.tmp([128, 8, 128], BF16)
    q32, E, L1, L2, g32, G, kin, Tm = T(), T(), T(), T(), T(), T(), T(), T()
    qg = [C.tmp([128, NT], BF16) for _ in range(2)]
    kd = [C.tmp([128, NT], BF16) for _ in range(2)]
    ks = C.tmp([128, NT], BF16)
    kstok = C.tmp([128, NT // 128, 128], BF16)
    vtok = C.tmp([128, NT // 128, 128], BF16)
    kdv = C.tmp([128, 64, NSLOT], F32)
    decrep = C.tmp([128, 64, NSLOT], F32)
    dec = [C.tmp([128, NSLOT], F32) for _ in range(2)]
    gtot = C.tmp([128, NT // 64], F32)
    sst = [C.tmp([128, 64, NSLOT], BF16) for _ in range(2)]
    sfin = C.tmp([128, 64], F32)
    dsum = C.tmp([128, 1], F32); dx = C.tmp([128, 1], F32)
    if final:
        wg = [C.tmp([128, 8, 64], BF16) for _ in range(2)]
        am = [C.tmp([128, 128], BF16) for _ in range(2)]
        osb = q32[0:64]; oln = E[0:64]; ors = L1[0:64]; gsl = L2[0:64]
        osq = g32[0:64].bitcast(BF16)[:, 0:512]
        yb = [G[0:64].bitcast(BF16)[:, 0:512], kin[0:64].bitcast(BF16)[:, 0:512]]
    for fc in range(4):
        S.dma(wch[0], W["cd_chunks"][fc].rearrange("(c p) n -> p c n", p=128), q="q_pool")
        S.dma(wv, W["w_v"][:, fc * 128:(fc + 1) * 128].rearrange("(c p) n -> p c n", p=128), q="q_pool")
        for t in range(NT // 128):
            pb = C.pbank[C.rr("hv", 2)]
            for c in range(8):
                S.matmul(pb[:, 0:128], C.hT[:, c, t * 128:(t + 1) * 128], wv[:, c, :], start=(c == 0), stop=(c == 7))
            S.copy(vtok[:, t, :], pb[:, 0:128], eng="act")
        for d in range(2):
            S.dma(wch[1 + d], W["cd_chunks"][4 + 4 * d + fc].rearrange("(c p) n -> p c n", p=128), q="q_pool")
        for d in range(2):
            lb = lbT[:, d * 4 + fc:d * 4 + fc + 1]
            for (g0, gn) in GROUPS:
                nchg = gn // 64
                pq = C.pbank[2]
                pz = C.pbank[3]
                for c in range(8):
                    S.matmul(pq[:, :gn], wch[0][:, c, :], C.hT[:, c, g0:g0 + gn], start=(c == 0), stop=(c == 7))
                for c in range(8):
                    S.matmul(pz[:, :gn], wch[1 + d][:, c, :], C.hT[:, c, g0:g0 + gn], start=(c == 0), stop=(c == 7))
                S.act(q32[:, :gn], pq[:, :gn], AF.Silu)
                S.act(E[:, :gn], pz[:, :gn], AF.Exp, scale=-1.0)
                S.act(L1[:, :gn], E[:, :gn], AF.Ln, scale=lb, bias=C.one_t)
                S.act(L2[:, :gn], E[:, :gn], AF.Ln, scale=1.0, bias=C.one_t)
                S.tt(g32[:, :gn], L1[:, :gn], L2[:, :gn], ALU.subtract)
                S.act(Tm[:, :gn], g32[:, :gn], AF.Exp)
                S.ts(kin[:, :gn], Tm[:, :gn], -1.0, ALU.mult, 1.0, ALU.add)
                S.op("dve", lambda gn=gn: nc.vector.tensor_tensor_scan(out=G[:, :gn], data0=m01[:, :gn], data1=g32[:, :gn],
                                                                     initial=0.0, op0=ALU.mult, op1=ALU.add),
                     outs=[G[:, :gn]], ins=[m01[:, :gn], g32[:, :gn]])
                G3 = G[:, :gn].rearrange("p (c t) -> p c t", t=64)
                tot = gtot[:, g0 // 64:g0 // 64 + nchg]
                S.copy(tot, G3[:, :, 63])
                if d == 1:
                    S.tt(Tm[:, :gn], g32[:, :gn], G[:, :gn], ALU.subtract)
                    S.tt(G3, Tm[:, :gn].rearrange("p (c t) -> p c t", t=64),
                         tot.unsqueeze(2).to_broadcast([128, nchg, 64]), ALU.add)
                S.act(Tm[:, :gn], G[:, :gn], AF.Exp)
                S.tt(qg[d][:, g0:g0 + gn], q32[:, :gn], Tm[:, :gn], ALU.mult)
                S.act(Tm[:, :gn], G[:, :gn], AF.Exp, scale=-1.0)
                S.tt(kd[d][:, g0:g0 + gn], kin[:, :gn], Tm[:, :gn], ALU.mult)
                S.tt(L1[:, :gn].rearrange("p (c t) -> p c t", t=64), tot.unsqueeze(2).to_broadcast([128, nchg, 64]),
                     G3, ALU.subtract)
                S.act(L2[:, :gn], L1[:, :gn], AF.Exp)
                S.tt(ks[:, g0:g0 + gn], kin[:, :gn], L2[:, :gn], ALU.mult)
            S.memset(dec[d], 0.0)
            S.memset(kdv[:, :, 0], 0.0)
            for c in range(NCH_X + NCH_C):
                s = slot_of(d, c)
                S.act(dec[d][:, s:s + 1], gtot[:, c:c + 1], AF.Exp)
            for t in range(NT // 128):
                pt = C.pbank[4 + C.rr("hpt", 2)].bitcast(BF16)
                S.transpose(pt[:, 0:128], ks[:, t * 128:(t + 1) * 128], C.ident_b)
                S.copy(kstok[:, t, :], pt[:, 0:128], eng="act")
            for c in range(NCH_X + NCH_C):
                t, cc = c // 2, c % 2
                pb = C.pbank[6 + C.rr("hkv", 2)]
                S.matmul(pb[:, 0:128], kstok[cc * 64:(cc + 1) * 64, t, :], vtok[cc * 64:(cc + 1) * 64, t, :])
                s = slot_of(d, c)
                S.copy(kdv[0:64, :, s], pb[0:64, 0:64])
                S.copy(kdv[64:128, :, s], pb[64:128, 64:128], eng="act")
            if not final:
                S.memset(kdv[:, :, XINIT], 0.0)
            else:
                S.dma(sfin, X["Sstart"][d, fc], q="q_sp")
                S.copy(kdv[:, :, XINIT], sfin)
            S.copy(decrep, dec[d].unsqueeze(1).to_broadcast([128, 64, NSLOT]))
            S.op("dve", lambda d=d: nc.vector.tensor_tensor_scan(
                out=sst[d].rearrange("p v s -> p (v s)"), data0=decrep.rearrange("p v s -> p (v s)"),
                data1=kdv.rearrange("p v s -> p (v s)"), initial=0.0, op0=ALU.mult, op1=ALU.add),
                 outs=[sst[d]], ins=[decrep, kdv])
            if not final:
                S.copy(sfin, sst[d][:, :, NCH_C])
                S.dma(X["Sc"][d, fc], sfin, q="q_sp")
                S.copy(sfin, sst[d][:, :, NSLOT - 1])
                S.dma(X["L"][d, fc], sfin, q="q_sp")
                S.op("dve", lambda: nc.vector.reduce_sum(out=dsum, in_=gtot[:, 0:NCH_X], axis=AX.X),
                     outs=[dsum], ins=[gtot[:, 0:NCH_X]])
                S.act(dx, dsum, AF.Exp)
                S.dma(X["Dx"][d, fc], dx, q="q_sp")
        if not final:
            continue
        for hp in range(2):
            h = fc * 2 + hp
            P0 = hp * 64
            wgb = wg[hp]
            S.dma(wgb, W["w_gate"][h].rearrange("(c p) n -> p c n", p=128), q="q_pool")
            for gi, (g0, gn) in enumerate(GROUPS[:4]):
                po = C.pbank[0 + C.rr("hpo", 2)]
                for tt_ in range(gn // 128):
                    t = g0 // 128 + tt_
                    ams = []
                    for d in range(2):
                        pa = C.pbank[2 + d]
                        S.matmul(pa[:, 0:128], kd[d][P0:P0 + 64, t * 128:(t + 1) * 128], qg[d][P0:P0 + 64, t * 128:(t + 1) * 128])
                        a = am[d]
                        S.tt(a, pa[:, 0:128], Mf if d == 0 else Mb, ALU.mult)
                        ams.append(a)
                    for d in range(2):
                        S.matmul(po[0:64, tt_ * 128:(tt_ + 1) * 128], vtok[:, t, P0:P0 + 64], ams[d],
                                 start=(d == 0), stop=False)
                    k = 0
                    for cc in range(2):
                        c = t * 2 + cc
                        col = tt_ * 128 + cc * 64
                        for d in range(2):
                            s = slot_of(d, c)
                            S.matmul(po[0:64, col:col + 64], sst[d][P0:P0 + 64, :, s - 1],
                                     qg[d][P0:P0 + 64, c * 64:(c + 1) * 64], start=False, stop=(k == 3))
                            k += 1
                S.copy(osb[:, :gn], po[0:64, :gn], eng="act")
                S.act(osq[:, :gn], po[0:64, :gn], AF.Square)
                pn = C.pbank[4 + C.rr("hpn", 2)]
                S.matmul(pn[0:64, :gn], ones64, osq[:, :gn])
                S.act(oln[:, :gn], pn[0:64, :gn], AF.Sqrt, scale=1.0 / 64, bias=C.eps_t[0:64])
                S.op("dve", lambda gn=gn: nc.vector.reciprocal(out=ors[:, :gn], in_=oln[:, :gn]),
                     outs=[ors[:, :gn]], ins=[oln[:, :gn]])
                pg = C.pbank[6 + C.rr("hpg", 2)]
                for c in range(8):
                    S.matmul(pg[0:64, :gn], wgb[:, c, :], C.hT[:, c, g0:g0 + gn], start=(c == 0), stop=(c == 7))
                S.act(gsl[:, :gn], pg[0:64, :gn], AF.Silu)
                S.stt(osb[:, :gn], osb[:, :gn], cgn[:, 0:1], ors[:, :gn], ALU.mult, ALU.mult)
                y = yb[C.rr("hyb", 2)]
                S.tt(y[:, :gn], osb[:, :gn], gsl[:, :gn], ALU.mult)
                S.dma(X["yT"][h * 64:(h + 1) * 64, g0:g0 + gn], y[:, :gn], q="q_sp")


NWROWS = 48
NKW = NWROWS * 64
NKN = NKW + NCTX


def head_norm_proj(C, w_dram, gain_ap, dst_fn, cols):
    S = C.S
    if not hasattr(C, "t_hw"):
        C.t_hw = [C.tmp([128, 8, 64], BF16) for _ in range(2)]
        C.t_hsq = C.tmp([64, 512], BF16)
        C.t_hln = C.tmp([64, 512], F32)
        C.t_hrs = C.tmp([64, 512], F32)
        C.t_hx = C.tmp([64, 512], F32)
        C.t_o64 = C.tmp([64, 64], BF16)
        S.memset(C.t_o64, 1.0)
    w = C.t_hw[C.rr("hw", 2)]
    S.dma(w, w_dram.rearrange("(c p) n -> p c n", p=128), q="q_pool")
    for (g0, gn) in cols:
        pb = C.pbank[C.rr("hnp", 2)]
        for c in range(8):
            S.matmul(pb[0:64, :gn], w[:, c, :], C.hT[:, c, g0:g0 + gn], start=(c == 0), stop=(c == 7))
        S.act(C.t_hsq[:, :gn], pb[0:64, :gn], AF.Square)
        pn = C.pbank[2 + C.rr("hnn", 2)]
        S.matmul(pn[0:64, :gn], C.t_o64, C.t_hsq[:, :gn])
        S.act(C.t_hln[:, :gn], pn[0:64, :gn], AF.Sqrt, scale=1.0 / 64, bias=C.eps_t[0:64])
        S.op("dve", lambda gn=gn, hrs=C.t_hrs, hln=C.t_hln: C.nc.vector.reciprocal(out=hrs[:, :gn], in_=hln[:, :gn]),
             outs=[C.t_hrs[:, :gn]], ins=[C.t_hln[:, :gn]])
        S.stt(C.t_hx[:, :gn], pb[0:64, :gn], gain_ap, C.t_hrs[:, :gn], ALU.mult, ALU.mult)
        S.copy(dst_fn(g0, gn), C.t_hx[:, :gn], eng="act")


def na_front(C, W, O):
    S = C.S
    gk = C.tmp([64, 1], F32); S.dma(gk, W["gkn"])
    kout = [C.tmp([64, NT], BF16) for _ in range(2)]
    for h in range(8):
        ko = kout[h % 2]
        head_norm_proj(C, W["w_dk"][h], gk[:, 0:1], lambda g0, gn, ko=ko: ko[:, g0:g0 + gn], GROUPS)
        S.dma(O["kT"][h], ko, q="q_sp")
    wv = C.tmp([128, 8, 512], BF16)
    S.dma(wv, W["w_dv"].rearrange("(c p) n -> p c n", p=128), q="q_pool")
    vt = [C.tmp([128, 8, 65], BF16) for _ in range(2)]
    for i in range(2):
        S.memset(vt[i], 1.0)
    for t in range(NT // 128):
        pb = C.pbank[4 + C.rr("nvp", 2)]
        for c in range(8):
            S.matmul(pb[:, 0:512], C.hT[:, c, t * 128:(t + 1) * 128], wv[:, c, :], start=(c == 0), stop=(c == 7))
        v = vt[t % 2]
        S.copy(v[:, :, 0:64], pb[:, 0:512].rearrange("p (h d) -> p h d", h=8))
        S.dma(O["v"][t * 128:(t + 1) * 128], v, q="q_sp")


def na_table_index(lr):
    if lr < 4:
        return lr, 2, 6
    if lr >= 28:
        return 5 + (lr - 28), 0, 6
    return 4, 2, 4


def na(C, W, X):
    S = C.S
    nc = C.nc
    gq = C.tmp([64, 1], F32); S.dma(gq, W["gqn"])
    ones_f = C.tmp([128, 64], F32); S.memset(ones_f, 1.0)
    qT = C.tmp([64, NTX], BF16)
    kw = C.tmp([64, NKN], BF16)
    vA = C.tmp([128, NKN // 128, 65], BF16)
    vB = C.tmp([128, NKW // 128, 65], BF16)
    tabs = C.tmp([128, 9, 8, 64], F32)
    sc = [C.tmp([128, 512], F32) for _ in range(2)]
    pT = [C.tmp([128, 512], BF16) for _ in range(2)]
    osb = C.tmp([64, 512], F32); rr_ = C.tmp([128, 512], F32)
    yb = [C.tmp([64, 512], BF16) for _ in range(2)]
    xg = [(g0, gn) for (g0, gn) in GROUPS if g0 < NTX]
    for h in range(8):
        head_norm_proj(C, W["w_dq"][h], gq[:, 0:1], lambda g0, gn: qT[:, g0:g0 + gn], xg)
        S.dma(kw, X["kwin"][h], q="q_sp")
        S.dma(vA, X["vwin"][0:NKN, h, :].rearrange("(t p) e -> p t e", p=128), q="q_pool")
        S.dma(vB, X["vwin"][64:64 + NKW, h, :].rearrange("(t p) e -> p t e", p=128), q="q_pool")
        for ti in range(9):
            S.dma(tabs[:, ti], W["bias"][h, ti], q="q_sp")
        for rb in range(4):
            po = C.pbank[4 + C.rr("napo", 2)]
            for r8 in range(8):
                lr = rb * 8 + r8
                ti, j0, nw = na_table_index(lr)
                ps = C.pbank[6 + C.rr("naps", 2)]
                ntile = nw + 2
                qcols = qT[:, lr * 64:(lr + 1) * 64]
                vts = []
                for j in range(nw):
                    krow = lr + 2 * (j0 + j)
                    S.matmul(ps[:, j * 64:(j + 1) * 64], kw[:, krow * 64:krow * 64 + 128], qcols)
                    vts.append(vA[:, krow // 2, :] if krow % 2 == 0 else vB[:, krow // 2, :])
                for j in range(2):
                    S.matmul(ps[:, (nw + j) * 64:(nw + j + 1) * 64], kw[:, NKW + j * 128:NKW + (j + 1) * 128], qcols)
                    vts.append(vA[:, NKW // 128 + j, :])
                s_ = sc[C.rr("nasc", 2)]
                p = pT[C.rr("napT", 2)]
                n = ntile * 64
                tab = tabs[:, ti].rearrange("p t q -> p (t q)")[:, 0:n]
                S.stt(s_[:, :n], ps[:, :n], 0.125, tab, ALU.mult, ALU.add)
                S.act(p[:, :n], s_[:, :n], AF.Exp)
                for j in range(ntile):
                    S.matmul(po[0:65, r8 * 64:(r8 + 1) * 64], vts[j], p[:, j * 64:(j + 1) * 64],
                             start=(j == 0), stop=(j == ntile - 1))
            g0 = rb * 512
            S.copy(osb, po[0:64, :], eng="act")
            S.op("dve", lambda po=po: nc.vector.reciprocal(out=rr_[64:65, :], in_=po[64:65, :]),
                 outs=[rr_[64:65, :]], ins=[po[64:65, :]])
            pbc = C.pbank[0 + C.rr("nabc", 2)]
            S.matmul(pbc[0:64, :], ones_f[64:65, 0:64], rr_[64:65, :])
            y = yb[C.rr("nayb", 2)]
            S.tt(y, osb, pbc[0:64, :], ALU.mult)
            S.dma(X["yT"][512 + h * 64:512 + (h + 1) * 64, g0:g0 + 512], y, q="q_sp")


BF = ml_dtypes.bfloat16
GRID_W = 64
THETA = 10000.0
NTX, NCTX = 2048, 256
NT = NTX + NCTX


def consts_np():
    return {"ident": np.eye(128, dtype=np.float32), "ones": np.ones((128, 128), np.float32),
            "blk64": np.kron(np.eye(2, dtype=np.float32), np.ones((64, 64), np.float32))}


def rope_1d(pos, dim):
    inv = THETA ** (-np.arange(0, dim, 2, dtype=np.float32) / dim)
    ang = pos.astype(np.float32)[:, None] * inv[None, :]
    ang = np.concatenate([ang, ang], -1)
    return np.cos(ang), np.sin(ang)


def axial_tables(tok, dim):
    cr, sr = rope_1d(tok // GRID_W, dim // 2)
    cc, sc = rope_1d(tok % GRID_W, dim // 2)
    return np.concatenate([cr, cc], -1), np.concatenate([sr, sc], -1)


def partner_sign(dim):
    h = dim // 2
    q = h // 2
    i = np.arange(dim)
    first = (i % h) < q
    partner = np.where(first, i + q, i - q)
    sign = np.where(first, -1.0, 1.0).astype(np.float32)
    return partner, sign


def rope_tables_fm(core):
    tok = core * NTX + np.arange(NTX)
    cA, sA = axial_tables(tok, 64)
    _, sgA = partner_sign(64)
    cosA = np.ones((128, NT), np.float32); sinA = np.zeros((128, NT), np.float32)
    cosA[:, :NTX] = np.tile(cA.T, (2, 1)); sinA[:, :NTX] = np.tile((sA * sgA[None, :]).T, (2, 1))
    cB, sB = axial_tables(tok, 32)
    _, sgB = partner_sign(32)
    cosB = np.ones((128, NT), np.float32); sinB = np.zeros((128, NT), np.float32)
    cosB[64:96, :NTX] = cB.T; sinB[64:96, :NTX] = (sB * sgB[None, :]).T
    return {"cosA": cosA, "sinA": sinA, "cosB": cosB, "sinB": sinB}


def ab_weights(ab_w_in, wukv, a_qn, a_kn, b_qn, b_kn, b_kvn):
    W = ab_w_in
    pA, _ = partner_sign(64)
    pB, _ = partner_sign(32)
    ch = np.zeros((31, 1024, 128), np.float32)
    for c in range(4):
        cols = np.arange(c * 128, (c + 1) * 128)
        ch[c] = W[:, cols]
        rc = (cols // 64) * 64 + pA[cols % 64]
        ch[4 + c] = W[:, rc]
    cols = 512 + np.arange(128)
    ch[8] = W[:, cols]
    ch[9] = W[:, 512 + (np.arange(128) // 64) * 64 + pA[np.arange(128) % 64]]
    for h in range(8):
        base = 768 + h * 96
        ch[10 + h, :, :96] = W[:, base:base + 96]
        ch[18 + h, :, :64] = W[:, base:base + 64]
        ch[18 + h, :, 64:96] = W[:, base + 64 + pB]
    ch[26] = W[:, 1536:1664]
    ch[27] = W[:, 1664:1792]
    ch[28, :, 64:96] = W[:, 1792:1824]
    ch[29, :, 64:96] = W[:, 1792 + pB]
    ch[30] = W[:, 640:768]
    wk = np.zeros((8, 256, 128), np.float32)
    wv = np.zeros((256, 512), np.float32)
    for h in range(8):
        wk[h, :, :64] = wukv[:, h * 128:h * 128 + 64]
        wv[:, h * 64:(h + 1) * 64] = wukv[:, h * 128 + 64:h * 128 + 128]
    gains = np.zeros((128, 16), np.float32)
    i = np.arange(128)
    gains[:, 0] = a_qn[i % 64]; gains[:, 1] = a_qn[pA[i % 64]]
    gains[:, 2] = a_kn[i % 64]; gains[:, 3] = a_kn[pA[i % 64]]
    pBfull = np.concatenate([np.arange(64), 64 + pB])
    gains[:96, 4] = b_qn; gains[:96, 5] = b_qn[pBfull]
    gains[:96, 6] = b_kn; gains[:96, 7] = b_kn[pBfull]
    gains[:, 8] = b_kvn[:128]; gains[:, 9] = b_kvn[128:]
    return {"ab_chunks": ch, "wukv_k": wk, "wukv_v": wv, "gains": gains}


def mods_layout(mod):
    m = mod.reshape(2, 2, 9, 8, 128)
    return np.ascontiguousarray(m.transpose(4, 0, 2, 3, 1))


def ng_layout(norm_g):
    return np.ascontiguousarray(norm_g.reshape(2, 3, 8, 128).transpose(3, 0, 1, 2))


def hgrn_weights(cd_w_in, hgrn_lb, c_gn):
    W = cd_w_in
    ch = np.zeros((12, 1024, 128), np.float32)
    for fc in range(4):
        ch[fc] = W[:, fc * 128:(fc + 1) * 128]
        ch[4 + fc] = W[:, 512 + fc * 128:512 + (fc + 1) * 128]
        ch[8 + fc] = W[:, 1024 + fc * 128:1024 + (fc + 1) * 128]
    wg = np.stack([W[:, 2048 + h * 64:2048 + (h + 1) * 64] for h in range(8)], 0)
    m01 = np.ones((128, 512), np.float32); m01[:, ::64] = 0.0
    i = np.arange(128)
    same = (i[:, None] // 64) == (i[None, :] // 64)
    Mf = (same & (i[:, None] <= i[None, :])).astype(np.float32)
    Mb = (same & (i[:, None] >= i[None, :])).astype(np.float32)
    return {"cd_chunks": ch, "w_v": np.ascontiguousarray(W[:, 1536:2048]), "w_gate": np.ascontiguousarray(wg),
            "hl": np.ascontiguousarray(hgrn_lb.reshape(2, 2, 4, 128).transpose(3, 0, 1, 2).reshape(128, 16)),
            "cgn": np.ascontiguousarray(c_gn.reshape(64, 1)), "m01": m01, "Mf": Mf, "Mb": Mb}


D_WIN_H, D_WIN_W, ROWS = 8, 16, 256


def na_tile_slots(lr):
    if lr < 4:
        return lr, 2, 6
    if lr >= 28:
        return 5 + (lr - 28), 0, 6
    return 4, 2, 4


def na_bias_tables(rpb, core):
    NEG = -30000.0
    out = np.zeros((8, 9, 128, 8, 64), np.float32)
    rep_rows = {0: 0, 1: 1, 2: 2, 3: 3, 4: 16, 5: 28, 6: 29, 7: 30, 8: 31}
    cq = np.arange(64)
    cs = np.clip(cq - D_WIN_W // 2, 0, GRID_W - D_WIN_W)
    for ti, lr in rep_rows.items():
        _, j0, nw = na_tile_slots(lr)
        r = core * 32 + lr
        rs = int(np.clip(r - D_WIN_H // 2, 0, ROWS - D_WIN_H))
        for j in range(nw):
            for half in range(2):
                kr = core * 32 + (lr - 8) + 2 * (j0 + j) + half
                kc = np.arange(64)
                okr = (rs <= kr < rs + D_WIN_H) and (0 <= kr < ROWS)
                blk = np.full((8, 64, 64), NEG, np.float32)
                if okr:
                    okc = (kc[:, None] >= cs[None, :]) & (kc[:, None] < cs[None, :] + D_WIN_W)
                    rel_r = kr - r + (D_WIN_H - 1)
                    rel_c = np.clip(kc[:, None] - cq[None, :] + (D_WIN_W - 1), 0, 2 * D_WIN_W - 2)
                    vals = rpb[:, rel_r][:, rel_c]
                    blk = np.where(okc[None], vals, NEG).astype(np.float32)
                out[:, ti, half * 64:(half + 1) * 64, j, :] = blk
    return out


def na_weights(cd_w_in, d_qn, d_kn):
    W = cd_w_in
    return {"w_dq": np.ascontiguousarray(np.stack([W[:, 2560 + h * 64:2560 + (h + 1) * 64] for h in range(8)], 0)),
            "w_dk": np.ascontiguousarray(np.stack([W[:, 3072 + h * 64:3072 + (h + 1) * 64] for h in range(8)], 0)),
            "w_dv": np.ascontiguousarray(W[:, 3584:4096]),
            "gqn": np.ascontiguousarray(d_qn.reshape(64, 1)), "gkn": np.ascontiguousarray(d_kn.reshape(64, 1))}


def hgrn_fold(C, Lall, Dall, Sc, masks, Sstart):
    S = C.S
    mk_ = C.tmp([128, 32], F32); S.dma(mk_, masks)
    P = C.tmp([128, 64], F32)
    Lj = [C.tmp([128, 64], F32) for _ in range(2)]
    Dj = [C.tmp([128, 1], F32) for _ in range(2)]
    Dm = C.tmp([128, 1], F32); Lm = C.tmp([128, 64], F32)
    for d in range(2):
        order = list(range(8)) if d == 0 else list(range(7, -1, -1))
        for fc in range(4):
            S.dma(P, Sc[d, fc], q="q_sp")
            for j in order:
                i = C.rr("fold", 2)
                S.dma(Lj[i], Lall[j, d, fc], q="q_sp")
                S.dma(Dj[i], Dall[j, d, fc], q="q_sp")
                m = mk_[:, d * 8 + j:d * 8 + j + 1]
                om = mk_[:, 16 + d * 8 + j:16 + d * 8 + j + 1]
                S.ts(Dm, Dj[i], m, ALU.mult, om, ALU.add)
                S.ts(Lm, Lj[i], m, ALU.mult)
                S.stt(P, P, Dm[:, 0:1], Lm, ALU.mult, ALU.add)
            S.dma(Sstart[d, fc], P, q="q_sp")


def _mk(nc):
    S = Sched(nc)
    di = lambda n, sh, dt=F32: nc.dram_tensor(n, list(sh), dt, kind="ExternalInput").ap()
    do = lambda n, sh, dt=F32: nc.dram_tensor(n, list(sh), dt, kind="ExternalOutput").ap()
    consts = {k: di(k, [128, 128]) for k in ("ident", "ones", "blk64")}
    return S, di, do, consts


def build_l1():
    nc = bass.Bass("TRN2", target_bir_lowering=False)
    S, di, do, consts = _mk(nc)
    x = di("x", [NTX, D]); ctx = di("ctx", [NCTX, D]); c = di("c", [1, D]); cc = di("c_ctx", [D])
    norm_g = di("norm_g", [1, 3, D]); ada_w = di("ada_w", [1, D, 9 * D]); ada_b = di("ada_b", [1, 9 * D])
    w1 = di("w1", [D, DFF]); w3 = di("w3", [D, DFF]); w2 = di("w2", [DFF, D])
    W = {"ab_chunks": di("ab_chunks", [31, 1024, 128]), "wukv_k": di("wukv_k", [8, 256, 128]),
         "wukv_v": di("wukv_v", [256, 512]), "gains": di("gains", [128, 16])}
    T = {k: di(k, [128, NT]) for k in ("cosA", "sinA", "cosB", "sinB")}
    O = {"qa": do("qa", [4, 128, NT], BF16), "ka": do("ka", [128, NT], BF16), "qb": do("qb", [8, 96, NT], BF16),
         "kb": do("kb", [8, 96, NT], BF16), "va": do("va", [NT, 2, 65], BF16), "vb": do("vb", [NT, 8, 65], BF16)}
    xTo = do("xT_out", [8, 128, NT]); modo = do("mod_out", [128, 144])
    C = Ctx(S, consts)
    load_xT(C, x, 0, NTX); load_xT(C, ctx, NTX, NCTX)
    C.phase("ada"); adaln(C, 0, c, cc, ada_w, ada_b, norm_g)
    S.dma(modo, C.modT.rearrange("p m c w -> p (m c w)"), q="q_pool")
    C.phase("ffn"); ffn(C, 0, w1, w3, w2)
    C.phase("ab"); ab_front(C, W, T, O)
    store_xT_fm(C, xTo)
    S.finalize()
    return nc


def build_l2():
    nc = bass.Bass("TRN2", target_bir_lowering=False)
    S, di, do, consts = _mk(nc)
    xTi = di("xT_in", [8, 128, NT]); mods = di("mods", [128, 2, 9, 8, 2]); ng = di("ng", [128, 2, 3, 8])
    A = {"qa": di("qa", [4, 128, NT], BF16), "ka": di("ka_all", [128, NK], BF16), "va": di("va_all", [NK, 2, 65], BF16),
         "qb": di("qb", [8, 96, NT], BF16), "kb": di("kb_all", [8, 96, NK], BF16), "vb": di("vb_all", [NK, 8, 65], BF16),
         "yT": do("yT0", [1024, NT], BF16)}
    wout = di("ab_w_out", [D, D])
    w1a = di("w1a", [D, DFF]); w3a = di("w3a", [D, DFF]); w2a = di("w2a", [DFF, D])
    c = di("c", [1, D]); cc = di("c_ctx", [D])
    norm_g = di("norm_g", [1, 3, D]); ada_w = di("ada_w", [1, D, 9 * D]); ada_b = di("ada_b", [1, 9 * D])
    w1b = di("w1b", [D, DFF]); w3b = di("w3b", [D, DFF]); w2b = di("w2b", [DFF, D])
    WH = {"cd_chunks": di("cd_chunks", [12, 1024, 128]), "w_v": di("w_v", [1024, 512]), "w_gate": di("w_gate", [8, 1024, 64]),
          "hl": di("hl", [128, 16]), "cgn": di("cgn", [64, 1]), "m01": di("m01", [128, 512]),
          "Mf": di("Mf", [128, 128]), "Mb": di("Mb", [128, 128])}
    WN = {"w_dk": di("w_dk", [8, 1024, 64]), "w_dv": di("w_dv", [1024, 512]), "gkn": di("gkn", [64, 1])}
    XH = {"L": do("L", [2, 4, 128, 64]), "Dx": do("Dx", [2, 4, 128, 1]), "Sc": do("Sc", [2, 4, 128, 64])}
    ON = {"kT": do("dkT", [8, 64, NT], BF16), "v": do("dv", [NT, 8, 65], BF16)}
    xTo = do("xT_out", [8, 128, NT]); modo = do("mod_out", [128, 144])
    C = Ctx(S, consts)
    load_xT_fm(C, xTi)
    load_mods(C, 0, mods, ng)
    C.phase("att"); attention(C, A)
    C.phase("oproj"); out_proj(C, A["yT"], wout)
    C.phase("ffn"); ffn(C, 2, w1a, w3a, w2a)
    C.phase("ada"); adaln(C, 0, c, cc, ada_w, ada_b, norm_g)
    S.dma(modo, C.modT.rearrange("p m c w -> p (m c w)"), q="q_pool")
    C.phase("ffn"); ffn(C, 0, w1b, w3b, w2b)
    C.phase("nm")
    for (g0, gn) in GROUPS:
        norm_mod(C, 1, g0, gn)
    C.phase("hg"); hgrn(C, WH, XH, False)
    C.phase("naf"); na_front(C, WN, ON)
    store_xT_fm(C, xTo)
    S.finalize()
    return nc


def build_l3():
    nc = bass.Bass("TRN2", target_bir_lowering=False)
    S, di, do, consts = _mk(nc)
    xTi = di("xT_in", [8, 128, NT]); mods = di("mods", [128, 2, 9, 8, 2]); ng = di("ng", [128, 2, 3, 8])
    WH = {"cd_chunks": di("cd_chunks", [12, 1024, 128]), "w_v": di("w_v", [1024, 512]), "w_gate": di("w_gate", [8, 1024, 64]),
          "hl": di("hl", [128, 16]), "cgn": di("cgn", [64, 1]), "m01": di("m01", [128, 512]),
          "Mf": di("Mf", [128, 128]), "Mb": di("Mb", [128, 128])}
    Lall = di("Lall", [8, 2, 4, 128, 64]); Dall = di("Dall", [8, 2, 4, 128, 1]); Sc = di("Sc", [2, 4, 128, 64])
    masks = di("masks", [128, 32])
    Sst = do("Sstart", [2, 4, 128, 64])
    yT = do("yT1", [1024, NT], BF16)
    WN = {"w_dq": di("w_dq", [8, 1024, 64]), "gqn": di("gqn", [64, 1]), "bias": di("bias", [8, 9, 128, 8, 64])}
    XN = {"kwin": di("kwin", [8, 64, NKN], BF16), "vwin": di("vwin", [NKN + 64, 8, 65], BF16), "yT": yT}
    wout = di("cd_w_out", [D, D])
    w1 = di("w1", [D, DFF]); w3 = di("w3", [D, DFF]); w2 = di("w2", [DFF, D])
    out = do("out", [NTX, D])
    C = Ctx(S, consts)
    load_xT_fm(C, xTi)
    load_mods(C, 1, mods, ng)
    C.phase("nm")
    for (g0, gn) in GROUPS:
        norm_mod(C, 1, g0, gn)
    C.phase("fold"); hgrn_fold(C, Lall, Dall, Sc, masks, Sst)
    C.phase("hg"); hgrn(C, WH, {"Sstart": Sst, "yT": yT}, True)
    C.phase("na"); na(C, WN, XN)
    C.phase("z")
    zt = C.tmp([128, NCTX], BF16); S.memset(zt, 0.0)
    for c8 in range(8):
        S.dma(yT[c8 * 128:(c8 + 1) * 128, NTX:NT], zt, q="q_pool")
    C.phase("oproj"); out_proj(C, yT, wout)
    C.phase("ffn"); ffn(C, 2, w1, w3, w2)
    C.phase("st"); store_xT(C, out, 0, NTX)
    S.finalize()
    return nc


def kernel(**inputs):
    n = 8
    f32 = lambda a: np.ascontiguousarray(np.asarray(a, dtype=np.float32))
    I = {k: f32(v) for k, v in inputs.items()}
    x = I["x"][0]; ctx = I["ctx"][0]
    cn = consts_np()
    ng_l = ng_layout(I["norm_g"])
    cores = list(range(n))
    abw = ab_weights(I["ab_w_in"][0], I["ab_b_wukv"][0], I["ab_a_qn"][0], I["ab_a_kn"][0], I["ab_b_qn"][0],
                     I["ab_b_kn"][0], I["ab_b_kvn"][0])
    sh1 = dict(cn, ctx=ctx, c=I["c"], c_ctx=I["c_ctx"], norm_g=I["norm_g"][0:1], ada_w=I["ada_w"][0:1],
               ada_b=I["ada_b"][0:1], w1=I["ffn_w1"][0, 0], w3=I["ffn_w3"][0, 0], w2=I["ffn_w2"][0, 0], **abw)
    maps = [dict(sh1, x=np.ascontiguousarray(x[i * NTX:(i + 1) * NTX]), **rope_tables_fm(i)) for i in cores]
    r1 = run_bass_kernel_spmd(build_l1(), maps, core_ids=cores).results
    mod0 = np.asarray(r1[0]["mod_out"], np.float32).reshape(128, 9, 8, 2)
    cat = np.concatenate
    ka_all = cat([np.asarray(r["ka"])[:, :NTX] for r in r1] + [np.asarray(r1[0]["ka"])[:, NTX:]], axis=1)
    kb_all = cat([np.asarray(r["kb"])[:, :, :NTX] for r in r1] + [np.asarray(r1[0]["kb"])[:, :, NTX:]], axis=2)
    va_all = cat([np.asarray(r["va"])[:NTX] for r in r1] + [np.asarray(r1[0]["va"])[NTX:]], axis=0)
    vb_all = cat([np.asarray(r["vb"])[:NTX] for r in r1] + [np.asarray(r1[0]["vb"])[NTX:]], axis=0)
    hw = hgrn_weights(I["cd_w_in"][0], I["hgrn_lb"], I["cd_c_gn"][0])
    nw = na_weights(I["cd_w_in"][0], I["cd_d_qn"][0], I["cd_d_kn"][0])
    mods = np.zeros((128, 2, 9, 8, 2), np.float32); mods[:, 0] = mod0
    sh2 = dict(cn, mods=mods, ng=ng_l, ka_all=np.ascontiguousarray(ka_all), kb_all=np.ascontiguousarray(kb_all),
               va_all=np.ascontiguousarray(va_all), vb_all=np.ascontiguousarray(vb_all), ab_w_out=I["ab_w_out"][0],
               w1a=I["ffn_w1"][0, 1], w3a=I["ffn_w3"][0, 1], w2a=I["ffn_w2"][0, 1], c=I["c"], c_ctx=I["c_ctx"],
               norm_g=I["norm_g"][1:2], ada_w=I["ada_w"][1:2], ada_b=I["ada_b"][1:2],
               w1b=I["ffn_w1"][1, 0], w3b=I["ffn_w3"][1, 0], w2b=I["ffn_w2"][1, 0],
               w_dk=nw["w_dk"], w_dv=nw["w_dv"], gkn=nw["gkn"], **hw)
    maps = [dict(sh2, xT_in=np.asarray(r1[i]["xT_out"]), qa=np.asarray(r1[i]["qa"]), qb=np.asarray(r1[i]["qb"])) for i in cores]
    r2 = run_bass_kernel_spmd(build_l2(), maps, core_ids=cores).results
    mods[:, 1] = np.asarray(r2[0]["mod_out"], np.float32).reshape(128, 9, 8, 2)
    Lall = np.stack([np.asarray(r["L"], np.float32) for r in r2], 0)
    Dall = np.stack([np.asarray(r["Dx"], np.float32) for r in r2], 0)
    zk = np.zeros((8, 64, 64), np.asarray(r2[0]["dkT"]).dtype)
    zv = np.zeros((64, 8, 65), np.asarray(r2[0]["dv"]).dtype)
    maps = []
    for i in cores:
        kparts, vparts = [], []
        for br in range(NWROWS):
            gr = i * 32 - 8 + br
            if 0 <= gr < 256:
                src, lr = gr // 32, gr % 32
                kparts.append(np.asarray(r2[src]["dkT"])[:, :, lr * 64:(lr + 1) * 64])
                vparts.append(np.asarray(r2[src]["dv"])[lr * 64:(lr + 1) * 64])
            else:
                kparts.append(zk); vparts.append(zv)
        kparts.append(np.asarray(r2[i]["dkT"])[:, :, NTX:]); vparts.append(np.asarray(r2[i]["dv"])[NTX:])
        vparts.append(zv)
        masks = np.zeros((128, 32), np.float32)
        for j in range(8):
            masks[:, j] = 1.0 if j < i else 0.0
            masks[:, 8 + j] = 1.0 if j > i else 0.0
        masks[:, 16:32] = 1.0 - masks[:, 0:16]
        maps.append(dict(cn, xT_in=np.asarray(r2[i]["xT_out"]), mods=mods, ng=ng_l, Lall=Lall, Dall=Dall,
                         Sc=np.asarray(r2[i]["Sc"]), masks=masks, w_dq=nw["w_dq"], gqn=nw["gqn"],
                         bias=na_bias_tables(I["cd_d_rpb"][0], i), kwin=np.ascontiguousarray(cat(kparts, axis=2)),
                         vwin=np.ascontiguousarray(cat(vparts, axis=0)), cd_w_out=I["cd_w_out"][0],
                         w1=I["ffn_w1"][1, 1], w3=I["ffn_w3"][1, 1], w2=I["ffn_w2"][1, 1], **hw))
    r3 = run_bass_kernel_spmd(build_l3(), maps, core_ids=cores).results
    out = cat([np.asarray(r["out"], dtype=np.float32) for r in r3], axis=0)
    return out[None]
```
